# Optimizing a Trainium2 kernel written in Bass

```python
import jax, jax.numpy as jnp
from jax import lax
import numpy as np

D_MODEL = 1024
BATCH = 2
SEQ = 16384
DEPTH = 4

GRID_W = 64
CTX_LEN = 256
HEAD_DIM = 64
NORM_EPS = 1e-6
POOL_WIDTH = D_MODEL // 4
POOL_WINDOWS = (2, 4, 8, 16)
POOL_GROUP = POOL_WIDTH // 4
GLA_WIDTH = D_MODEL // 4
GLA_DK = 64
GLA_DV = 64
GLA_HEADS = GLA_WIDTH // GLA_DV
GLA_GATE_RANK = 16
GLA_GATE_NORM = 16.0
GLA_CHUNK = 64
ATTN_WIDTH = D_MODEL // 2
ATTN_Q_HEADS = ATTN_WIDTH // HEAD_DIM
ATTN_KV_HEADS = ATTN_Q_HEADS // 4
ATTN_KV_WIDTH = ATTN_KV_HEADS * HEAD_DIM
Q_BLOCK = 128
ROPE_THETA = 10000.0
MLP_HIDDEN = 4 * D_MODEL
IN_SIZES = (POOL_WIDTH,
            GLA_HEADS * GLA_DK, GLA_HEADS * GLA_DK,
            GLA_WIDTH, GLA_WIDTH,
            GLA_GATE_RANK, GLA_GATE_RANK,
            ATTN_WIDTH, ATTN_KV_WIDTH, ATTN_KV_WIDTH)
IN_WIDTH = sum(IN_SIZES)
MIX_WIDTH = POOL_WIDTH + GLA_WIDTH + ATTN_WIDTH

kernel_name = "hybrid_pool_gla_gqa_prefix_dit"


def rmsnorm(x, g):
    xf = x.astype(jnp.float32)
    y = xf * lax.rsqrt(jnp.mean(xf * xf, axis=-1, keepdims=True) + NORM_EPS)
    return (y * g).astype(x.dtype)


def modulation(cvec, w_mod, b_mod):
    m = (jax.nn.silu(cvec) @ w_mod + b_mod)[..., None, :]
    return jnp.split(m, 6, axis=-1)


def split_in(u):
    idx = np.cumsum(np.array(IN_SIZES))[:-1].tolist()
    return jnp.split(u, idx, axis=-1)


def pool_mixer(p, w_pool, s_pool):
    B, N, _ = p.shape
    pg = p.reshape(B, N, 4, POOL_GROUP)
    pf = pg.astype(jnp.float32)
    P = jnp.concatenate([jnp.zeros((B, 1, 4, POOL_GROUP), jnp.float32), jnp.cumsum(pf, axis=1)], axis=1)
    t = np.arange(N)
    outs = []
    for gi, w in enumerate(POOL_WINDOWS):
        lo = np.clip(t - w // 2, 0, N - 1)
        hi = np.clip(t + w // 2 - 1, 0, N - 1)
        cnt = jnp.asarray((hi - lo + 1)[None, :, None], jnp.float32)
        Pg = P[:, :, gi]
        mean = (Pg[:, hi + 1] - Pg[:, lo]) / cnt
        outs.append(mean - pf[:, :, gi])
    m = jnp.stack(outs, axis=2)
    y = jnp.einsum('bngc,gcd->bngd', m, w_pool.astype(jnp.float32))
    return (y.reshape(B, N, POOL_WIDTH) * s_pool).astype(p.dtype)


def gla_features(q, k, v, r_f, r_b, w_gate, b_gate):
    B, N, _ = q.shape
    f32 = jnp.float32
    heads = lambda a: a.astype(f32).reshape(B, N, GLA_HEADS, -1)
    wg = w_gate.astype(f32)
    bg = b_gate.astype(f32)
    la_f = jax.nn.log_sigmoid(r_f.astype(f32) @ wg[0] + bg[0]) / GLA_GATE_NORM
    la_b = jax.nn.log_sigmoid(r_b.astype(f32) @ wg[1] + bg[1]) / GLA_GATE_NORM
    return heads(q) * GLA_DK ** -0.5, heads(k), heads(v), heads(la_f), heads(la_b)


def gla_chunk_scan(q, k, v, la, s0, strict):
    B, N, H, _ = q.shape
    C = GLA_CHUNK
    nc = N // C
    to_chunks = lambda a: a.reshape(B, nc, C, H, a.shape[-1]).transpose(1, 0, 3, 2, 4)
    mask = np.tril(np.ones((C, C), bool), k=-1 if strict else 0)[:, :, None]

    def step(S, inp):
        qc, kc, vc, lc = inp
        b = jnp.cumsum(lc, axis=2)
        o_inter = jnp.einsum('bhid,bhde->bhie', qc * jnp.exp(b), S)
        diff = b[:, :, :, None, :] - b[:, :, None, :, :]
        decay = jnp.exp(jnp.where(mask, diff, -jnp.inf))
        A = jnp.einsum('bhid,bhjd,bhijd->bhij', qc, kc, decay)
        o = o_inter + jnp.einsum('bhij,bhje->bhie', A, vc)
        bl = b[:, :, -1:, :]
        S_new = jnp.exp(bl[:, :, 0, :, None]) * S + jnp.einsum('bhjd,bhje->bhde', kc * jnp.exp(bl - b), vc)
        return S_new, o

    S, o = lax.scan(step, s0, (to_chunks(q), to_chunks(k), to_chunks(v), to_chunks(la)))
    return o.transpose(1, 0, 3, 2, 4).reshape(B, N, H, -1), S


def gla_bidir(q, k, v, la_f, la_b, s_f, s_b):
    o_f, s_f_new = gla_chunk_scan(q, k, v, la_f, s_f, strict=False)
    rev = lambda a: a[:, ::-1]
    o_b, s_b_new = gla_chunk_scan(rev(q), rev(k), rev(v), rev(la_b), s_b, strict=True)
    return o_f + rev(o_b), s_f_new, s_b_new


def gla_output(o, g, g_gla, out_dtype):
    B, N = o.shape[:2]
    gate = jax.nn.silu(g.astype(jnp.float32)).reshape(B, N, GLA_HEADS, GLA_DV)
    return (rmsnorm(o, g_gla) * gate).reshape(B, N, GLA_WIDTH).astype(out_dtype)


def attn_qkv(q, k, v, g_q, g_k):
    B, N, _ = q.shape
    q = rmsnorm(q.reshape(B, N, ATTN_Q_HEADS, HEAD_DIM), g_q)
    k = rmsnorm(k.reshape(B, N, ATTN_KV_HEADS, HEAD_DIM), g_k)
    return q, k, v.reshape(B, N, ATTN_KV_HEADS, HEAD_DIM)


def axial_rope(x, rows, cols):
    half = HEAD_DIM // 2
    inv = ROPE_THETA ** (-jnp.arange(0, half, 2, dtype=jnp.float32) / half)

    def rot(xh, pos):
        ang = pos.astype(jnp.float32)[:, None] * inv
        cos = jnp.cos(ang)[None, :, None, :]
        sin = jnp.sin(ang)[None, :, None, :]
        x1, x2 = jnp.split(xh.astype(jnp.float32), 2, axis=-1)
        return jnp.concatenate([x1 * cos - x2 * sin, x1 * sin + x2 * cos], axis=-1)

    xr, xc = jnp.split(x, 2, axis=-1)
    return jnp.concatenate([rot(xr, rows), rot(xc, cols)], axis=-1).astype(x.dtype)


def gqa_attend(qb, k, v):
    s = jnp.einsum('bqhgd,bkhd->bhgqk', qb, k, preferred_element_type=jnp.float32) * HEAD_DIM ** -0.5
    p = jax.nn.softmax(s, axis=-1)
    return jnp.einsum('bhgqk,bkhd->bqhgd', p.astype(v.dtype), v)


def latent_attention(q, k_all, v_all):
    B, N = q.shape[:2]
    G = ATTN_Q_HEADS // ATTN_KV_HEADS
    nb = N // Q_BLOCK
    qb = q.reshape(B, nb, Q_BLOCK, ATTN_KV_HEADS, G, HEAD_DIM).swapaxes(0, 1)
    ob = lax.map(lambda blk: gqa_attend(blk, k_all, v_all), qb)
    return ob.swapaxes(0, 1).reshape(B, N, ATTN_WIDTH)


def context_attention(q, k, v):
    B, L = q.shape[:2]
    G = ATTN_Q_HEADS // ATTN_KV_HEADS
    return gqa_attend(q.reshape(B, L, ATTN_KV_HEADS, G, HEAD_DIM), k, v).reshape(B, L, ATTN_WIDTH)


def squared_relu_mlp(h, w_up, w_down):
    return jnp.square(jax.nn.relu(h @ w_up)) @ w_down


def trunk_layer(x, cx, c, c_ctx, rows, cols, w_mod, b_mod, g_mix, w_in, w_pool, s_pool, w_gate, b_gate,
                g_gla, g_q, g_k, w_out, g_mlp, w_up, w_down, update_ctx):
    sh1, sc1, ga1, sh2, sc2, ga2 = modulation(c, w_mod, b_mod)
    sh1c, sc1c, ga1c, sh2c, sc2c, ga2c = modulation(c_ctx[None], w_mod, b_mod)

    u = (rmsnorm(x, g_mix) * (1 + sc1) + sh1) @ w_in
    uc = (rmsnorm(cx, g_mix) * (1 + sc1c) + sh1c) @ w_in
    p, gq, gk, gv, gg, rf, rb, aq, ak, av = split_in(u)
    pc, gqc, gkc, gvc, ggc, rfc, rbc, aqc, akc, avc = split_in(uc)

    B = x.shape[0]
    s_zero = jnp.zeros((B, GLA_HEADS, GLA_DK, GLA_DV), jnp.float32)
    o_gc, s_f, s_b = gla_bidir(*gla_features(gqc, gkc, gvc, rfc, rbc, w_gate, b_gate), s_zero, s_zero)
    o_gl, _, _ = gla_bidir(*gla_features(gq, gk, gv, rf, rb, w_gate, b_gate), s_f, s_b)

    qc_, kc_, vc_ = attn_qkv(aqc, akc, avc, g_q, g_k)
    ql, kl, vl = attn_qkv(aq, ak, av, g_q, g_k)
    ql = axial_rope(ql, rows, cols)
    kl = axial_rope(kl, rows, cols)
    k_all = jnp.concatenate([kc_, kl], axis=1)
    v_all = jnp.concatenate([vc_, vl], axis=1)

    mix = jnp.concatenate([pool_mixer(p, w_pool, s_pool),
                           gla_output(o_gl, gg, g_gla, x.dtype),
                           latent_attention(ql, k_all, v_all)], axis=-1) @ w_out
    x = x + ga1 * mix
    x = x + ga2 * squared_relu_mlp(rmsnorm(x, g_mlp) * (1 + sc2) + sh2, w_up, w_down)

    if update_ctx:
        mix_c = jnp.concatenate([pool_mixer(pc, w_pool, s_pool),
                                 gla_output(o_gc, ggc, g_gla, cx.dtype),
                                 context_attention(qc_, kc_, vc_)], axis=-1) @ w_out
        cx = cx + ga1c * mix_c
        cx = cx + ga2c * squared_relu_mlp(rmsnorm(cx, g_mlp) * (1 + sc2c) + sh2c, w_up, w_down)
    return x, cx


def setup_inputs(seed: int = 0) -> dict:
    key = jax.random.key(seed)
    ks = jax.random.split(key, 20)
    D = D_MODEL
    nrm = lambda k, shape, scale: jax.random.normal(k, shape, jnp.float32) * scale
    return {
        "x": nrm(ks[0], (BATCH, SEQ, D), 1.0),
        "c": nrm(ks[1], (BATCH, D), 1.0),
        "ctx": nrm(ks[2], (BATCH, CTX_LEN, D), 1.0),
        "c_ctx": nrm(ks[3], (D,), 1.0),
        "w_mod": nrm(ks[4], (DEPTH, D, 6 * D), 0.5 * D ** -0.5),
        "b_mod": nrm(ks[5], (DEPTH, 6 * D), 0.01),
        "g_mix": 1.0 + nrm(ks[6], (DEPTH, D), 0.05),
        "w_in": nrm(ks[7], (DEPTH, D, IN_WIDTH), D ** -0.5),
        "w_pool": nrm(ks[8], (DEPTH, 4, POOL_GROUP, POOL_GROUP), POOL_GROUP ** -0.5),
        "s_pool": 1.0 + nrm(ks[9], (DEPTH, POOL_WIDTH), 0.1),
        "w_gate": nrm(ks[10], (DEPTH, 2, GLA_GATE_RANK, GLA_HEADS * GLA_DK), GLA_GATE_RANK ** -0.5),
        "b_gate": nrm(ks[11], (DEPTH, 2, GLA_HEADS * GLA_DK), 0.1),
        "g_gla": 1.0 + nrm(ks[12], (DEPTH, GLA_DV), 0.05),
        "g_q": 1.0 + nrm(ks[13], (DEPTH, HEAD_DIM), 0.05),
        "g_k": 1.0 + nrm(ks[14], (DEPTH, HEAD_DIM), 0.05),
        "w_out": nrm(ks[15], (DEPTH, MIX_WIDTH, D), MIX_WIDTH ** -0.5),
        "g_mlp": 1.0 + nrm(ks[16], (DEPTH, D), 0.05),
        "w_up": nrm(ks[17], (DEPTH, D, MLP_HIDDEN), D ** -0.5),
        "w_down": nrm(ks[18], (DEPTH, MLP_HIDDEN, D), MLP_HIDDEN ** -0.5),
        "g_final": 1.0 + nrm(ks[19], (D,), 0.05),
    }


def reference(x, c, ctx, c_ctx, w_mod, b_mod, g_mix, w_in, w_pool, s_pool, w_gate, b_gate,
              g_gla, g_q, g_k, w_out, g_mlp, w_up, w_down, g_final):
    n_tok = x.shape[1]
    ROWS = n_tok // GRID_W
    rows = jnp.repeat(jnp.arange(ROWS, dtype=jnp.int32), GRID_W)
    cols = jnp.tile(jnp.arange(GRID_W, dtype=jnp.int32), ROWS)
    cx = ctx
    for l in range(DEPTH):
        x, cx = trunk_layer(x, cx, c, c_ctx, rows, cols, w_mod[l], b_mod[l], g_mix[l], w_in[l],
                            w_pool[l], s_pool[l], w_gate[l], b_gate[l], g_gla[l], g_q[l], g_k[l],
                            w_out[l], g_mlp[l], w_up[l], w_down[l], update_ctx=(l < DEPTH - 1))
    return rmsnorm(x, g_final)
```

```python
from contextlib import ExitStack
import numpy as np
import ml_dtypes
import concourse.bass as bass
import concourse.mybir as mybir
from concourse.bass_utils import run_bass_kernel_spmd

F32 = mybir.dt.float32
BF16 = mybir.dt.bfloat16
AF = mybir.ActivationFunctionType
ALU = mybir.AluOpType
AX = mybir.AxisListType
ENGS = ("pe", "act", "dve", "pool", "sp")

D = 1024
CTX = 256
EPS = 1e-6
GROUPS = [[0, 1, 2, 3], [4, 5, 6, 7]]
WIN = (2, 4, 8, 16)


class Sched:
    def __init__(self, nc, n_dma_sems=8, same_engine_wait=True):
        self.nc = nc
        self.eobj = {"pe": nc.tensor, "act": nc.scalar, "dve": nc.vector, "pool": nc.gpsimd, "sp": nc.sync}
        self.prog = {e: [] for e in ENGS}
        self.sems = {}
        self.cnt = {}
        self.known = {e: {} for e in ENGS}
        self.last_w = {}
        self.last_r = {}
        self.n_dma_sems = n_dma_sems
        self.dma_rr = {e: 0 for e in ENGS}
        self.same_engine_wait = same_engine_wait
        self.nops = 0

    def _sem(self, key):
        if key not in self.sems:
            self.sems[key] = self.nc.alloc_semaphore(name="s_" + key)
            self.cnt[key] = 0
        return self.sems[key]

    def _deps(self, eng, reads, writes):
        deps = {}
        for k in reads:
            for sk, v in self.last_w.get(k, {}).items():
                deps[sk] = max(deps.get(sk, 0), v)
        for k in writes:
            for sk, v in self.last_w.get(k, {}).items():
                deps[sk] = max(deps.get(sk, 0), v)
            for sk, v in self.last_r.get(k, {}).items():
                deps[sk] = max(deps.get(sk, 0), v)
        waits = []
        kn = self.known[eng]
        for sk, v in deps.items():
            if sk == eng and (eng == "pe" or not self.same_engine_wait):
                continue
            if kn.get(sk, 0) >= v:
                continue
            kn[sk] = v
            waits.append((sk, v))
        return waits

    def _record(self, tok, reads, writes):
        sk, v = tok
        for k in reads:
            d = self.last_r.setdefault(k, {})
            d[sk] = max(d.get(sk, 0), v)
        for k in writes:
            self.last_w[k] = {sk: v}
            self.last_r[k] = {}

    def op(self, eng, fn, reads=(), writes=()):
        self._sem(eng)
        waits = self._deps(eng, reads, writes)
        self.cnt[eng] += 1
        self._record((eng, self.cnt[eng]), reads, writes)
        self.prog[eng].append((waits, fn, (eng, 1)))
        self.nops += 1

    def _ext(self, eng, sk, inc, fn, reads, writes):
        self._sem(sk)
        waits = self._deps(eng, reads, writes)
        prev = self.cnt[sk]
        if prev > 0 and self.known[eng].get(sk, 0) < prev:
            self.known[eng][sk] = prev
            waits.append((sk, prev))
        self.cnt[sk] += inc
        self._record((sk, self.cnt[sk]), reads, writes)
        self.prog[eng].append((waits, fn, (sk, inc)))
        self.nops += 1

    def dma(self, eng, fn, reads=(), writes=()):
        j = self.dma_rr[eng]
        self.dma_rr[eng] = (j + 1) % self.n_dma_sems
        self._ext(eng, "d%s%d" % (eng, j), 16, fn, reads, writes)

    def cc(self, fn, reads=(), writes=()):
        self._ext("pool", "cc", 1, fn, reads, writes)

    def final_wait(self, eng, keys):
        waits = self._deps(eng, keys, ())
        self.prog[eng].append((waits, None, None))

    def emit(self):
        nc = self.nc
        sems = self.sems
        for q in ("sp", "pool"):
            waits = []
            for sk, v in self.cnt.items():
                if (sk.startswith("d" + q) or (q == "pool" and sk == "cc")) and v > 0 and self.known[q].get(sk, 0) < v:
                    self.known[q][sk] = v
                    waits.append((sk, v))
            if waits:
                self.prog[q].append((waits, None, None))
        prog = self.prog

        def run(ename):
            eo = self.eobj[ename]
            for waits, fn, inc in prog[ename]:
                for sk, v in waits:
                    eo.wait_ge(sems[sk], v)
                if fn is not None:
                    ins = fn()
                    ins.then_inc(sems[inc[0]], inc[1])

        with nc.Block() as block:
            @block.sync
            def _(e):
                run("sp")

            @block.tensor
            def _(e):
                run("pe")

            @block.scalar
            def _(e):
                run("act")

            @block.vector
            def _(e):
                run("dve")

            @block.gpsimd
            def _(e):
                run("pool")
        self.prog = {e: [] for e in ENGS}


class Ring:
    def __init__(self, tensors, name):
        self.t = tensors
        self.name = name
        self.i = -1

    def next(self):
        self.i = (self.i + 1) % len(self.t)
        return self.t[self.i], "%s%d" % (self.name, self.i)


def build(TL, NL, debug=(), stop=None):
    assert TL % 512 == 0
    T = CTX + TL
    NK = CTX + 4 * TL
    NKT = NK // 128
    NCH = T // 64
    LG = [(CTX + 512 * i, 512, False) for i in range(TL // 512)]
    GRPS = [(0, CTX, True)] + LG
    PE0 = 0
    PE1 = CTX + 16
    PEW = CTX + 16 + TL + 16

    nc = bass.Bass("TRN2", target_bir_lowering=False)
    dbg = set(debug)

    def din(name, shape, dt=F32):
        return nc.dram_tensor(name, list(shape), dt, kind="ExternalInput").ap()

    def dscr(name, shape, dt=F32):
        if name in dbg:
            return nc.dram_tensor(name, list(shape), dt, kind="ExternalOutput").ap()
        return nc.dram_tensor(name, list(shape), dt).ap()

    x_in = din("x", [TL, D])
    ctx_in = din("ctx", [CTX, D])
    cvec = din("cvec", [2, D])
    w_mod = din("w_mod", [NL, D, 6 * D])
    b_mod = din("b_mod", [NL, 6 * D])
    g_mix = din("g_mix", [NL, D])
    w_in = din("w_in", [NL, D, 2080])
    w_pool = din("w_pool", [NL, 4, 64, 64])
    s_pool = din("s_pool", [NL, 256])
    w_gate = din("w_gate", [NL, 2, 16, 256])
    b_gate = din("b_gate", [NL, 2, 256])
    g_gla = din("g_gla", [NL, 64])
    g_q = din("g_q", [NL, 64])
    g_k = din("g_k", [NL, 64])
    w_out = din("w_out", [NL, D, D])
    g_mlp = din("g_mlp", [NL, D])
    w_up = din("w_up", [NL, D, 4 * D])
    w_down = din("w_down", [NL, 4 * D, D])
    g_final = din("g_final", [1, D])
    rope_c = din("rope_c", [TL, 64])
    rope_s = din("rope_s", [TL, 64])
    pool_rc = din("pool_rc", [2, 128, T])
    sel = din("sel", [128, 12])
    out = nc.dram_tensor("out", [TL, D], F32, kind="ExternalOutput").ap()

    x_d = dscr("x_d", [T, D])
    mod_d = dscr("mod_d", [2, 6 * D])
    pTe_d = dscr("pTe_d", [256, PEW])
    gq_d = dscr("gq_d", [256, T])
    gk_d = dscr("gk_d", [256, T])
    la_d = dscr("la_d", [2, 256, T])
    gv_d = dscr("gv_d", [T, 256], BF16)
    gg_d = dscr("gg_d", [T, 256])
    QT_d = dscr("QT_d", [128, 4, T], BF16)
    KTc_d = dscr("KTc_d", [128, CTX], BF16)
    Vc_d = dscr("Vc_d", [CTX, 132], BF16)
    NG = TL // 512
    ccK_in = [dscr("ccK_in%d" % g, [128, 512], BF16) for g in range(NG)]
    ccK_out = [dscr("ccK_out%d" % g, [512, 512], BF16) for g in range(NG)]
    ccV_in = [dscr("ccV_in%d" % g, [512, 132], BF16) for g in range(NG)]
    ccV_out = [dscr("ccV_out%d" % g, [2048, 132], BF16) for g in range(NG)]
    ccS_in = dscr("ccS_in", [128, 320])
    ccS_out = dscr("ccS_out", [512, 320])
    mixT_d = dscr("mixT_d", [D, T], BF16)
    h2T_d = dscr("h2T_d", [D, T], BF16)

    import itertools
    uid = itertools.count()
    S = Sched(nc)
    EO = S.eobj

    def tk(name, t0, n=128):
        return ["%s@%d" % (name, i) for i in range(t0 // 128, (t0 + n + 127) // 128)]

    def OP(eng, fn, r=(), w=()):
        S.op(eng, fn, r, w)

    def DMA(q, o, i, r=(), w=(), **kw):
        S.dma(q, lambda: EO[q].dma_start(out=o, in_=i, **kw), r, w)

    def TT(eng, o, a, b, op, r, w):
        OP(eng, lambda: EO[eng].tensor_tensor(out=o, in0=a, in1=b, op=op), r, w)

    def TS(eng, o, a, s1, s2, op0, op1, r, w):
        OP(eng, lambda: EO[eng].tensor_scalar(out=o, in0=a, scalar1=s1, scalar2=s2, op0=op0, op1=op1), r, w)

    def STT(o, a, s, b, op0, op1, r, w):
        OP("dve", lambda: nc.vector.scalar_tensor_tensor(out=o, in0=a, scalar=s, in1=b, op0=op0, op1=op1), r, w)

    def ACT(o, i, func, r, w, **kw):
        OP("act", lambda: nc.scalar.activation(out=o, in_=i, func=func, **kw), r, w)

    def CP(eng, o, i, r, w):
        if eng == "act":
            ACT(o, i, AF.Copy, r, w)
        else:
            OP(eng, lambda: EO[eng].tensor_copy(out=o, in_=i), r, w)

    def MM(o, lhsT, rhs, start, stop, r, w):
        OP("pe", lambda: nc.tensor.matmul(o, lhsT=lhsT, rhs=rhs, start=start, stop=stop), r, w)

    def TR(o, i, ident, r, w):
        OP("pe", lambda: nc.tensor.transpose(out=o, in_=i, identity=ident), r, w)

    def MEMSET(eng, ap, val, w):
        OP(eng, lambda: EO[eng].memset(ap, val), (), w)

    psF = nc.alloc_psum_tensor("psF", [128, 6 * 512], F32)
    psB = nc.alloc_psum_tensor("psB", [128, 2 * 1024], BF16)

    def PF(i):
        return psF[:, i * 512:(i + 1) * 512]

    def PB(i):
        return psB[:, i * 1024:(i + 1) * 1024]

    GA = nc.alloc_sbuf_tensor
    idb = GA("idb", [128, 128], BF16)
    idf = GA("idf", [128, 128], F32)
    maskf = GA("maskf", [64, 64], F32)
    maskb = GA("maskb", [64, 64], F32)
    rmask = GA("rmask", [128, 512], F32)
    csT = GA("csT", [128, 8, 2], F32)
    selt = GA("selt", [128, 12], F32)
    sctx = GA("sctx", [128, 2, 2, 64], F32)
    sloc = GA("sloc", [128, 2, 2, 64], F32)
    sin_ = GA("sin", [128, 2, 2, 64], F32)
    btot = GA("btot", [128, 2, 2], F32)
    lnq = GA("lnq", [128, 1], F32)

    for nm, t in (("idb", idb), ("idf", idf)):
        MEMSET("pool", t[:], 1.0, [nm])
        OP("pool", lambda t=t: nc.gpsimd.affine_select(out=t[:], in_=t[:], pattern=[[-1, 128]], compare_op=ALU.is_equal,
                                                        fill=0.0, base=0, channel_multiplier=1), [nm], [nm])
    MEMSET("pool", maskf[:], 1.0, ["maskf"])
    OP("pool", lambda: nc.gpsimd.affine_select(out=maskf[:], in_=maskf[:], pattern=[[1, 64]], compare_op=ALU.is_ge,
                                                fill=0.0, base=0, channel_multiplier=-1), ["maskf"], ["maskf"])
    MEMSET("pool", maskb[:], 1.0, ["maskb"])
    OP("pool", lambda: nc.gpsimd.affine_select(out=maskb[:], in_=maskb[:], pattern=[[-1, 64]], compare_op=ALU.is_gt,
                                                fill=0.0, base=0, channel_multiplier=1), ["maskb"], ["maskb"])
    MEMSET("dve", lnq[:], float(np.log(0.125)), ["lnq"])
    MEMSET("dve", rmask[:], 1.0, ["rmask"])
    MEMSET("dve", rmask[:].rearrange("p (c t) -> p c t", t=64)[:, :, 0:1], 0.0, ["rmask"])
    for r_ in range(2):
        DMA("sp", csT[:, :, r_], cvec[r_, :].rearrange("(k p) -> p k", p=128), (), ["csT"], allow_slow_non_contiguous=True)
    ACT(csT[:], csT[:], AF.Silu, ["csT"], ["csT"])
    DMA("sp", selt[:], sel[:, :], (), ["selt"])
    DMA("sp", x_d[0:CTX, :], ctx_in[:, :], (), tk("x_d", 0, CTX))
    DMA("sp", x_d[CTX:T, :], x_in[:, :], (), tk("x_d", CTX, TL))
    with ExitStack() as st:
        z = st.enter_context(nc.sbuf_tensor("zinit", [128, 2, 8], F32))
        MEMSET("dve", z[:], 0.0, ["z"])
        for t in range(2):
            DMA("sp", pTe_d[t * 128:(t + 1) * 128, 0:8], z[:, t, :], ["z"], ["pTe_d"])
            DMA("sp", pTe_d[t * 128:(t + 1) * 128, 8 + CTX:16 + CTX], z[:, t, :], ["z"], ["pTe_d"])
        S.emit()

    def rms_rstd(st_, ss, rstd, n, key):
        ACT(rstd, ss, AF.Sqrt, [key + "ss"], [key + "r"], scale=1.0 / n, bias=EPS)
        OP("dve", lambda: nc.vector.reciprocal(out=rstd, in_=rstd), [key + "r"], [key + "r"])

    def bcast_load(t, row_ap, key, q="sp"):
        DMA(q, t[:], row_ap.partition_broadcast(128), ["mod_d"], [key])

    for l in range(NL):
        last = (l == NL - 1)
        with ExitStack() as st:
            A = lambda n, s, d: st.enter_context(nc.sbuf_tensor("%s_u%d" % (n, next(uid)), s, d))
            wm = Ring([A("wm%d" % i, [128, 3072], F32) for i in range(2)], "wm")
            modsb = A("modsb", [2, 6 * D], F32)
            bm = A("bm", [2, 6 * D], F32)
            gmx = A("gmx", [2, 2, D], F32)
            DMA("sp", bm[:], b_mod[l, :].partition_broadcast(2), (), ["bm"])
            DMA("sp", gmx[:, 0, :], g_mix[l, :].partition_broadcast(2), (), ["gmx"])
            DMA("sp", gmx[:, 1, :], g_mlp[l, :].partition_broadcast(2), (), ["gmx"])
            for hf in range(2):
                for k in range(8):
                    w_, wk = wm.next()
                    DMA("sp", w_[:], w_mod[l, k * 128:(k + 1) * 128, hf * 3072:(hf + 1) * 3072], (), [wk])
                    for b in range(6):
                        MM(PF(b)[0:2, :], csT[:, k, :], w_[:, b * 512:(b + 1) * 512], k == 0, k == 7,
                           ["csT", wk], ["pf%d" % b])
                for b in range(6):
                    c0 = hf * 3072 + b * 512
                    TT("dve", modsb[:, c0:c0 + 512], PF(b)[0:2, :], bm[:, c0:c0 + 512], ALU.add,
                       ["pf%d" % b, "bm"], ["modsb"])
            for j, c0 in ((0, D), (1, 4 * D)):
                STT(modsb[:, c0:c0 + D], modsb[:, c0:c0 + D], 1.0, gmx[:, j, :], ALU.add, ALU.mult,
                    ["modsb", "gmx"], ["modsb"])
            DMA("sp", mod_d[:, :], modsb[:], ["modsb"], ["mod_d"])
            S.emit()
        if stop == "M":
            return nc, S

        with ExitStack() as st:
            A = lambda n, s, d: st.enter_context(nc.sbuf_tensor("%s_u%d" % (n, next(uid)), s, d))
            win = A("win", [128, 8, 2080], BF16)
            for k in range(8):
                DMA("pool", win[:, k, :], w_in[l, k * 128:(k + 1) * 128, :], (), ["win"])
            bc = {}
            for si in range(2):
                for nm, c0 in (("gm1", D), ("sh1", 0)):
                    t = A("%s_%d" % (nm, si), [128, D], F32)
                    bcast_load(t, mod_d[si, c0:c0 + D], "%s_%d" % (nm, si))
                    bc[(nm, si)] = t
            wg = A("wg", [16, 2, 256], F32)
            DMA("sp", wg[:], w_gate[l].rearrange("d r c -> r d c"), (), ["wg"])
            nbg = A("nbg", [128, 4], F32)
            for d_ in range(2):
                DMA("sp", nbg[:, 2 * d_:2 * d_ + 2], b_gate[l, d_, :].rearrange("(t p) -> p t", p=128), (), ["nbg"],
                    allow_slow_non_contiguous=True)
            TS("dve", nbg[:], nbg[:], -1.0, None, ALU.mult, ALU.bypass, ["nbg"], ["nbg"])
            gqb = A("gqb", [128, 64], F32)
            gkb = A("gkb", [128, 64], F32)
            DMA("sp", gqb[:], g_q[l, :].partition_broadcast(128), (), ["gqb"])
            DMA("sp", gkb[:], g_k[l, :].partition_broadcast(128), (), ["gkb"])
            xr = Ring([A("xa%d" % i, [128, D], F32) for i in range(3)], "xa")
            junk = A("junk", [128, D], BF16)
            st2 = A("st2", [128, 8], F32)
            h1 = A("h1", [128, D], F32)
            hb = A("hb", [128, D], BF16)
            hTr = Ring([A("hT%d" % i, [128, 8, 512], BF16) for i in range(2)], "hT")
            stg = Ring([A("stg%d" % i, [128, 512], F32) for i in range(3)], "stg")
            rT = [A("rT%d" % i, [16, 512], F32) for i in range(2)]
            etmp = Ring([A("etmp%d" % i, [128, 512], F32) for i in range(2)], "etmp")
            sqs = A("sqs", [128, 512], F32)
            qn1 = A("qn1", [128, 512], F32)
            qn2 = A("qn2", [128, 512], F32)
            t1 = A("t1", [128, 512], F32)
            t2 = A("t2", [128, 512], F32)
            ssq = A("ssq", [128, 16], F32)
            qb = A("qb", [128, 512], BF16)
            kb = A("kb", [128, 128], BF16)
            qTs = Ring([A("qTs%d" % i, [128, 512], BF16) for i in range(2)], "qTs")
            kTs = Ring([A("kTs%d" % i, [128, 128], BF16) for i in range(2)], "kTs")
            vst = Ring([A("vst%d" % i, [128, 132], BF16) for i in range(2)], "vst")
            for i in range(2):
                MEMSET("dve", vst.t[i][:], 0.0, ["vst%d" % i])
                MEMSET("dve", vst.t[i][:, 64:65], 1.0, ["vst%d" % i])
                MEMSET("dve", vst.t[i][:, 130:131], 1.0, ["vst%d" % i])
            gvs = Ring([A("gvs%d" % i, [128, 256], BF16) for i in range(2)], "gvs")
            ggs = Ring([A("ggs%d" % i, [128, 256], F32) for i in range(2)], "ggs")
            rcs = Ring([A("rcs%d" % i, [128, 2, 64], F32) for i in range(2)], "rcs")

            def headnorm_rope(ps, nh, gb, dst_view, is_ctx, rck, tagk):
                W_ = nh * 64
                v3 = lambda a: a.rearrange("p (h d) -> p h d", d=64)
                ACT(sqs[:, 0:W_], ps, AF.Square, [tagk], ["sqs"])
                OP("dve", lambda: nc.vector.tensor_reduce(out=ssq[:, 0:nh], in_=v3(sqs[:, 0:W_]), axis=AX.X, op=ALU.add),
                   ["sqs"], ["ssqss"])
                rms_rstd(None, ssq[:, 0:nh], ssq[:, 8:8 + nh], 64, "ssq")
                TT("dve", v3(qn1[:, 0:W_]), v3(ps), ssq[:, 8:8 + nh].unsqueeze(2).broadcast_to([128, nh, 64]), ALU.mult,
                   [tagk, "ssqr"], ["qn1"])
                TT("pool", v3(qn2[:, 0:W_]), v3(qn1[:, 0:W_]), gb[:, :].unsqueeze(1).broadcast_to([128, nh, 64]), ALU.mult,
                   ["qn1", "gqb", "gkb"], ["qn2"])
                if is_ctx:
                    src = qn2
                    srck = "qn2"
                else:
                    rc_, rk = rck
                    TT("dve", v3(t1[:, 0:W_]), v3(qn2[:, 0:W_]), rc_[:, 0, :].unsqueeze(1).broadcast_to([128, nh, 64]), ALU.mult,
                       ["qn2", rk], ["t1"])
                    v5 = lambda a: a.rearrange("p (h a b c) -> p h a b c", a=2, b=2, c=16)
                    sg5 = rc_[:, 1, :].rearrange("p (a b c) -> p a b c", a=2, b=2, c=16)
                    for bb in range(2):
                        TT("pool", v5(t2[:, 0:W_])[:, :, :, bb, :], v5(qn2[:, 0:W_])[:, :, :, 1 - bb, :],
                           sg5[:, :, bb, :].unsqueeze(1).broadcast_to([128, nh, 2, 16]), ALU.mult,
                           ["qn2", rk], ["t2_%d" % bb])
                    TT("dve", t1[:, 0:W_], t1[:, 0:W_], t2[:, 0:W_], ALU.add, ["t1", "t2_0", "t2_1"], ["t1"])
                    src = t1
                    srck = "t1"
                return src, srck

            for (t0, n, is_ctx) in GRPS:
                si = 1 if is_ctx else 0
                nt = n // 128
                hT, hk = hTr.next()
                for j in range(nt):
                    xt, xk = xr.next()
                    r0 = t0 + j * 128
                    DMA("sp", xt[:], x_d[r0:r0 + 128, :], tk("x_d", r0), [xk])
                    MEMSET("pool", st2[:, 0:1], 0.0, ["st2ss"])
                    ACT(junk[:], xt[:], AF.Square, [xk, "st2ss"], ["junk", "st2ss"], accum_out=st2[:, 0:1])
                    rms_rstd(None, st2[:, 0:1], st2[:, 1:2], D, "st2")
                    STT(h1[:], xt[:], st2[:, 1:2], bc[("gm1", si)][:], ALU.mult, ALU.mult, [xk, "st2r", "gm1_%d" % si], ["h1"])
                    TT("pool", hb[:], h1[:], bc[("sh1", si)][:], ALU.add, ["h1", "sh1_%d" % si], ["hb"])
                    for k in range(8):
                        TR(PB(0)[:, k * 128:(k + 1) * 128], hb[:, k * 128:(k + 1) * 128], idb[:], ["hb", "idb"], ["pb0"])
                    CP("act", hT[:, :, j * 128:(j + 1) * 128], PB(0).rearrange("p (k t) -> p k t", k=8), ["pb0"], [hk])
                fm = [("p", 0, 0), ("p", 1, 128), ("gq", 0, 256), ("gq", 1, 384), ("gk", 0, 512), ("gk", 1, 640)]
                for i, (nm, tl_, c0) in enumerate(fm):
                    pk = "pf%d" % (i % 3)
                    ps = PF(i % 3)
                    for k in range(8):
                        MM(ps[:, 0:n], win[:, k, c0:c0 + 128], hT[:, k, 0:n], k == 0, k == 7, ["win", hk], [pk])
                    sg, sk_ = stg.next()
                    CP("act" if i % 2 == 0 else "dve", sg[:, 0:n], ps[:, 0:n], [pk], [sk_])
                    if nm == "p":
                        e0 = (PE0 if is_ctx else PE1 - CTX) + 8 + t0
                        DMA("sp", pTe_d[tl_ * 128:(tl_ + 1) * 128, e0:e0 + n], sg[:, 0:n], [sk_], ["pTe_d"])
                    else:
                        dd = gq_d if nm == "gq" else gk_d
                        DMA("sp", dd[tl_ * 128:(tl_ + 1) * 128, t0:t0 + n], sg[:, 0:n], [sk_], [nm + "_d"])
                for d_ in range(2):
                    c0 = 1280 + 16 * d_
                    pk = "pf%d" % d_
                    for k in range(8):
                        MM(PF(d_)[0:16, 0:n], win[:, k, c0:c0 + 16], hT[:, k, 0:n], k == 0, k == 7, ["win", hk], [pk])
                    CP("dve", rT[d_][:, 0:n], PF(d_)[0:16, 0:n], [pk], ["rT%d" % d_])
                for d_ in range(2):
                    for pr in range(2):
                        pi = 3 + (d_ * 2 + pr) % 3
                        pk = "pf%d" % pi
                        MM(PF(pi)[:, 0:n], wg[:, d_, pr * 128:(pr + 1) * 128], rT[d_][:, 0:n], True, True,
                           ["wg", "rT%d" % d_], [pk])
                        e_, ek = etmp.next()
                        ACT(e_[:, 0:n], PF(pi)[:, 0:n], AF.Exp, [pk, "nbg"], [ek], scale=-1.0,
                            bias=nbg[:, d_ * 2 + pr:d_ * 2 + pr + 1])
                        ACT(e_[:, 0:n], e_[:, 0:n], AF.Ln, [ek], [ek], bias=1.0)
                        TS("dve", e_[:, 0:n], e_[:, 0:n], -1.0 / 16.0, None, ALU.mult, ALU.bypass, [ek], [ek])
                        DMA("sp", la_d[d_, pr * 128:(pr + 1) * 128, t0:t0 + n], e_[:, 0:n], [ek], ["la_d"])
                for j in range(nt):
                    r0 = t0 + j * 128
                    lhs = lambda k: hT[:, k, j * 128:(j + 1) * 128]
                    for k in range(8):
                        MM(PF(0), lhs(k), win[:, k, 768:1280], k == 0, k == 7, ["win", hk], ["pf0"])
                    gv_, gvk = gvs.next()
                    gg_, ggk = ggs.next()
                    CP("dve", gv_[:], PF(0)[:, 0:256], ["pf0"], [gvk])
                    ACT(gg_[:], PF(0)[:, 256:512], AF.Silu, ["pf0"], [ggk])
                    DMA("sp", gv_d[r0:r0 + 128, :], gv_[:], [gvk], tk("gv_d", r0))
                    DMA("sp", gg_d[r0:r0 + 128, :], gg_[:], [ggk], tk("gg_d", r0))
                    rck = None
                    if not is_ctx:
                        rc_, rk = rcs.next()
                        DMA("sp", rc_[:, 0, :], rope_c[r0 - CTX:r0 - CTX + 128, :], (), [rk])
                        DMA("sp", rc_[:, 1, :], rope_s[r0 - CTX:r0 - CTX + 128, :], (), [rk])
                        rck = (rc_, rk)
                    for k in range(8):
                        MM(PF(1), lhs(k), win[:, k, 1312:1824], k == 0, k == 7, ["win", hk], ["pf1"])
                    src, srck = headnorm_rope(PF(1), 8, gqb, None, is_ctx, rck, "pf1")
                    CP("pool", qb[:].rearrange("p (g k d) -> p k g d", g=4, k=2),
                       src[:, 0:512].rearrange("p (k g d) -> p k g d", k=2, g=4), [srck], ["qb"])
                    for g in range(4):
                        TR(PB(1)[:, g * 128:(g + 1) * 128], qb[:, g * 128:(g + 1) * 128], idb[:], ["qb", "idb"], ["pb1"])
                    qT_, qTk = qTs.next()
                    CP("act", qT_[:], PB(1)[:, 0:512], ["pb1"], [qTk])
                    DMA("sp", QT_d[:, :, r0:r0 + 128], qT_[:].rearrange("p (g t) -> p g t", g=4), [qTk], tk("QT_d", r0))
                    for k in range(8):
                        MM(PF(2)[:, 0:256], lhs(k), win[:, k, 1824:2080], k == 0, k == 7, ["win", hk], ["pf2"])
                    v_, vk = vst.next()
                    CP("act", v_[:].rearrange("p (h e) -> p h e", h=2)[:, :, 0:64],
                       PF(2)[:, 128:256].rearrange("p (h d) -> p h d", h=2), ["pf2"], [vk])
                    src, srck = headnorm_rope(PF(2)[:, 0:128], 2, gkb, None, is_ctx, rck, "pf2")
                    CP("dve", kb[:], src[:, 0:128], [srck], ["kb"])
                    TR(PB(1)[:, 512:640], kb[:], idb[:], ["kb", "idb"], ["pb1"])
                    kT_, kTk = kTs.next()
                    CP("act", kT_[:], PB(1)[:, 512:640], ["pb1"], [kTk])
                    if is_ctx:
                        DMA("sp", KTc_d[:, r0:r0 + 128], kT_[:], [kTk], ["KTc_d"])
                        DMA("sp", Vc_d[r0:r0 + 128, :], v_[:], [vk], ["Vc_d"])
                    else:
                        gi = (t0 - CTX) // 512
                        DMA("sp", ccK_in[gi][:, j * 128:(j + 1) * 128], kT_[:], [kTk], tk("ccK_in", r0))
                        DMA("sp", ccV_in[gi][j * 128:(j + 1) * 128, :], v_[:], [vk], tk("ccV_in", r0))
                if not is_ctx:
                    gi = (t0 - CTX) // 512
                    S.cc(lambda gi=gi: nc.gpsimd.collective_compute("AllGather", ALU.bypass, replica_groups=GROUPS,
                                                                    ins=[ccK_in[gi][:, :]], outs=[ccK_out[gi][:, :]]),
                         tk("ccK_in", t0, n), ["ccK_out%d" % gi])
                    S.cc(lambda gi=gi: nc.gpsimd.collective_compute("AllGather", ALU.bypass, replica_groups=GROUPS,
                                                                    ins=[ccV_in[gi][:, :]], outs=[ccV_out[gi][:, :]]),
                         tk("ccV_in", t0, n), ["ccV_out%d" % gi])
            S.emit()
        if stop == "A":
            return nc, S

        with ExitStack() as st:
            A = lambda n, s, d: st.enter_context(nc.sbuf_tensor("%s_u%d" % (n, next(uid)), s, d))
            oacc = A("oacc", [64, NCH, 256], F32)
            MEMSET("pool", oacc[:], 0.0, ["oacc"])
            Sst = [A("Sst%d" % d_, [128, 2, 64], F32) for d_ in range(2)]
            Sbf = [A("Sbf%d" % d_, [128, 2, 64], BF16) for d_ in range(2)]
            W = {}
            for d_ in range(2):
                for nm, shp, dt in (("la", [128, 2, 512], F32), ("gq", [128, 2, 512], F32), ("gk", [128, 2, 512], F32),
                                    ("b", [128, 2, 512], F32), ("eb", [128, 2, 512], F32), ("enb", [128, 2, 512], F32),
                                    ("ebl", [128, 2, 8], F32), ("qtb", [128, 2, 8, 2, 64], BF16), ("kt", [128, 2, 512], BF16),
                                    ("kh", [128, 2, 512], BF16), ("khat", [64, 8, 256], BF16), ("v", [128, 8, 256], BF16),
                                    ("AT", [128, 4, 64], BF16), ("tot", [128, 2, 8], F32)):
                    W[(nm, d_)] = A("%s%d" % (nm, d_), shp, dt)
                for nm in ("qtb", "v", "AT"):
                    MEMSET("pool", W[(nm, d_)][:], 0.0, ["%s%d" % (nm, d_)])
            K = lambda nm, d_: "%s%d" % (nm, d_)

            def set_state(d_, src, srck):
                if src is None:
                    MEMSET("dve", Sst[d_][:], 0.0, [K("S", d_)])
                else:
                    CP("dve", Sst[d_][:], src, [srck], [K("S", d_)])
                CP("act", Sbf[d_][:], Sst[d_][:], [K("S", d_)], [K("Sb", d_)])

            def prep(d_, t0, n, with_out, acc_btot):
                nch = n // 64
                la, gq, gk, b, eb, enb, ebl = (W[(x, d_)] for x in ("la", "gq", "gk", "b", "eb", "enb", "ebl"))
                qtb, kt, kh, khat, v = (W[(x, d_)] for x in ("qtb", "kt", "kh", "khat", "v"))
                c4 = lambda a: a[:, :, 0:n].rearrange("p t (c s) -> p t c s", s=64)
                DMA("sp", la[:, :, 0:n], la_d[d_, :, t0:t0 + n].rearrange("(t p) s -> p t s", p=128), ["la_d"], [K("la", d_)])
                DMA("sp", gk[:, :, 0:n], gk_d[:, t0:t0 + n].rearrange("(t p) s -> p t s", p=128), ["gk_d"], [K("gk", d_)])
                DMA("sp", v[0:64, 0:nch, :], gv_d[t0:t0 + n, :].rearrange("(c p) f -> p c f", p=64), tk("gv_d", t0, n), [K("v", d_)])
                if with_out:
                    DMA("sp", gq[:, :, 0:n], gq_d[:, t0:t0 + n].rearrange("(t p) s -> p t s", p=128), ["gq_d"], [K("gq", d_)])
                for pr in range(2):
                    OP("dve", lambda pr=pr: nc.vector.tensor_tensor_scan(out=b[:, pr, 0:n], data0=rmask[:, 0:n], data1=la[:, pr, 0:n],
                                                                      initial=0.0, op0=ALU.mult, op1=ALU.add),
                       ["rmask", K("la", d_)], [K("b%d" % pr, d_)])
                bk = [K("b0", d_), K("b1", d_)]
                ACT(ebl[:, :, 0:nch], c4(b)[:, :, :, 63], AF.Exp, bk, [K("ebl", d_)])
                if acc_btot:
                    for pr in range(2):
                        OP("dve", lambda pr=pr: nc.vector.tensor_reduce(out=eb[:, pr, 0:1], in_=la[:, pr, 0:n], axis=AX.X, op=ALU.add),
                           [K("la", d_)], [K("eb", d_)])
                    TT("dve", btot[:, d_, :], btot[:, d_, :], eb[:, :, 0], ALU.add, [K("eb", d_), "btot"], ["btot"])
                if d_ == 1:
                    tot = W[("tot", d_)]
                    CP("dve", tot[:, :, 0:nch], c4(b)[:, :, :, 63], bk, [K("tot", d_)])
                    for pr in range(2):
                        TT("dve", eb[:, pr, 0:n], la[:, pr, 0:n], b[:, pr, 0:n], ALU.subtract, [K("la", d_), bk[pr], K("eb", d_)], [K("eb", d_)])
                        TT("dve", c4(b)[:, pr], c4(eb)[:, pr], tot[:, pr, 0:nch].unsqueeze(2).broadcast_to([128, nch, 64]), ALU.add,
                           [K("eb", d_), K("tot", d_)], [bk[pr]])
                if with_out:
                    ACT(eb[:, :, 0:n], b[:, :, 0:n], AF.Exp, bk + [K("eb", d_), "lnq"], [K("eb", d_)], bias=lnq[:, 0:1])
                ACT(enb[:, :, 0:n], b[:, :, 0:n], AF.Exp, bk, [K("enb", d_)], scale=-1.0)
                if with_out:
                    for hh in range(2):
                        ps_ = slice(64 * hh, 64 * hh + 64)
                        TT("dve", qtb[ps_, :, 0:nch, hh, :], c4(gq)[ps_], c4(eb)[ps_], ALU.mult, [K("gq", d_), K("eb", d_)], [K("qtb", d_)])
                    TT("pool", kt[:, :, 0:n], gk[:, :, 0:n], enb[:, :, 0:n], ALU.mult, [K("gk", d_), K("enb", d_)], [K("kt", d_)])
                for pr in range(2):
                    TT("dve", c4(enb)[:, pr], c4(enb)[:, pr], ebl[:, pr, 0:nch].unsqueeze(2).broadcast_to([128, nch, 64]), ALU.mult,
                       [K("enb", d_), K("ebl", d_), K("kt", d_)], [K("enb", d_)])
                TT("pool", kh[:, :, 0:n], gk[:, :, 0:n], enb[:, :, 0:n], ALU.mult, [K("gk", d_), K("enb", d_)], [K("kh", d_)])
                for pr in range(2):
                    for c in range(nch):
                        TR(PB(d_)[0:64, c * 128:(c + 1) * 128], kh[:, pr, c * 64:(c + 1) * 64], idb[:], [K("kh", d_), "idb"], ["pb%d" % d_])
                    CP("act", khat[:, 0:nch, pr * 128:(pr + 1) * 128], PB(d_)[0:64, 0:nch * 128].rearrange("p (c f) -> p c f", f=128),
                       ["pb%d" % d_], [K("khat", d_)])

            def step(d_, c, gc, with_out):
                qtb, kt, khat, v, ebl, AT = (W[(x, d_)] for x in ("qtb", "kt", "khat", "v", "ebl", "AT"))
                cs = slice(c * 64, (c + 1) * 64)
                pa = 2 * d_
                pu = 2 * d_ + 1
                if with_out:
                    for pr in range(2):
                        MM(PF(pa)[0:64, pr * 128:(pr + 1) * 128], kt[:, pr, cs], qtb[:, pr, c, :, :], True, True,
                           [K("kt", d_), K("qtb", d_)], ["pf%d" % pa])
                    msk = maskf if d_ == 0 else maskb
                    TT("dve", AT[0:64], PF(pa)[0:64, 0:256].rearrange("p (h i) -> p h i", h=4),
                       msk[:, :].unsqueeze(1).broadcast_to([64, 4, 64]), ALU.mult, ["pf%d" % pa, "maskf", "maskb"], [K("AT", d_)])
                    for h in range(4):
                        pr, hh = h // 2, h % 2
                        MM(PF(pa)[0:64, 256 + h * 64:256 + (h + 1) * 64], AT[:, h, :], v[:, c, h * 64:(h + 1) * 64], True, False,
                           [K("AT", d_), K("v", d_)], ["pf%do" % pa])
                        MM(PF(pa)[0:64, 256 + h * 64:256 + (h + 1) * 64], qtb[:, pr, c, hh, :], Sbf[d_][:, pr, :], False, True,
                           [K("qtb", d_), K("Sb", d_)], ["pf%do" % pa])
                    TT("dve", oacc[:, gc, :], PF(pa)[0:64, 256:512], oacc[:, gc, :], ALU.add, ["pf%do" % pa, "oacc"], ["oacc"])
                for pr in range(2):
                    MM(PF(pu)[:, pr * 128:(pr + 1) * 128], khat[:, c, pr * 128:(pr + 1) * 128], v[0:64, c, pr * 128:(pr + 1) * 128], True, True,
                       [K("khat", d_), K("v", d_)], ["pf%d" % pu])
                for h in range(4):
                    pr, bs = h // 2, 64 * (h % 2)
                    STT(Sst[d_][bs:bs + 64, pr, :], Sst[d_][bs:bs + 64, pr, :], ebl[bs:bs + 64, pr, c:c + 1],
                        PF(pu)[bs:bs + 64, pr * 128 + (h % 2) * 64:pr * 128 + (h % 2) * 64 + 64], ALU.mult, ALU.add,
                        [K("S", d_), K("ebl", d_), "pf%d" % pu], [K("S", d_)])
                CP("act", Sbf[d_][:], Sst[d_][:], [K("S", d_)], [K("Sb", d_)])

            def scan(grps, with_out, acc_btot):
                order = {0: [(g, c) for g in grps for c in range(g[1] // 64)],
                         1: [(g, c) for g in reversed(grps) for c in reversed(range(g[1] // 64))]}
                cur = {0: None, 1: None}
                for i in range(len(order[0])):
                    for d_ in range(2):
                        g, c = order[d_][i]
                        if cur[d_] != g:
                            prep(d_, g[0], g[1], with_out, acc_btot)
                            cur[d_] = g
                        step(d_, c, g[0] // 64 + c, with_out)

            MEMSET("dve", btot[:], 0.0, ["btot"])
            for d_ in range(2):
                set_state(d_, None, None)
            scan([GRPS[0]], True, False)
            for d_ in range(2):
                CP("dve", sctx[:, d_], Sst[d_][:], [K("S", d_)], ["sctx"])
            if stop == "G1":
                S.emit()
                return nc, S
            for d_ in range(2):
                set_state(d_, None, None)
            scan(LG, False, True)
            for d_ in range(2):
                CP("dve", sloc[:, d_], Sst[d_][:], [K("S", d_)], ["sloc"])
            if stop == "G2":
                S.emit()
                return nc, S
            summ = A("summ", [128, 320], F32)
            MEMSET("pool", summ[:], 0.0, ["summ"])
            CP("dve", summ[:, 0:256], sloc[:].rearrange("p d t e -> p (d t e)"), ["sloc"], ["summ"])
            ACT(summ[:, 256:260], btot[:].rearrange("p d t -> p (d t)"), AF.Exp, ["btot", "summ"], ["summ"])
            for t in range(2):
                DMA("sp", summ[:, 260 + 8 * t:268 + 8 * t], pTe_d[t * 128:(t + 1) * 128, PE1 + 8:PE1 + 16], ["pTe_d", "summ"], ["summ"])
                DMA("sp", summ[:, 276 + 8 * t:284 + 8 * t], pTe_d[t * 128:(t + 1) * 128, PE1 + TL:PE1 + TL + 8], ["pTe_d", "summ"], ["summ"])
            DMA("sp", ccS_in[:, :], summ[:], ["summ"], ["ccS_in"])
            S.cc(lambda: nc.gpsimd.collective_compute("AllGather", ALU.bypass, replica_groups=GROUPS,
                                                      ins=[ccS_in[:, :]], outs=[ccS_out[:, :]]), ["ccS_in"], ["ccS_out"])
            gath = A("gath", [128, 4, 320], F32)
            DMA("sp", gath[:], ccS_out.rearrange("(r p) c -> p r c", p=128), ["ccS_out"], ["gath"])
            Tst = A("Tst", [128, 2, 64], F32)
            hal = A("hal", [128, 2, 2, 8], F32)
            MEMSET("dve", sin_[:], 0.0, ["sin"])
            MEMSET("dve", hal[:], 0.0, ["hal"])
            for d_ in range(2):
                CP("dve", Tst[:], sctx[:, d_], ["sctx"], ["Tst"])
                ranks = range(4) if d_ == 0 else range(3, -1, -1)
                for r in ranks:
                    STT(sin_[:, d_], Tst[:], selt[:, r:r + 1], sin_[:, d_], ALU.mult, ALU.add, ["Tst", "selt", "sin"], ["sin"])
                    dec = gath[:, r, 256 + 2 * d_:258 + 2 * d_]
                    TT("dve", Tst[:], Tst[:], dec.unsqueeze(2).broadcast_to([128, 2, 64]), ALU.mult, ["Tst", "gath"], ["Tst"])
                    TT("dve", Tst[:], Tst[:], gath[:, r, 128 * d_:128 * d_ + 128].rearrange("p (t e) -> p t e", t=2), ALU.add,
                       ["Tst", "gath"], ["Tst"])
            for r in range(4):
                STT(hal[:, 0], gath[:, r, 276:292].rearrange("p (t e) -> p t e", t=2), selt[:, 4 + r:5 + r], hal[:, 0],
                    ALU.mult, ALU.add, ["gath", "selt", "hal"], ["hal"])
                STT(hal[:, 1], gath[:, r, 260:276].rearrange("p (t e) -> p t e", t=2), selt[:, 8 + r:9 + r], hal[:, 1],
                    ALU.mult, ALU.add, ["gath", "selt", "hal"], ["hal"])
            for t in range(2):
                DMA("sp", pTe_d[t * 128:(t + 1) * 128, PE1:PE1 + 8], hal[:, 0, t, :], ["hal"], ["pTe_d"])
                DMA("sp", pTe_d[t * 128:(t + 1) * 128, PE1 + 8 + TL:PE1 + 16 + TL], hal[:, 1, t, :], ["hal"], ["pTe_d"])
            if stop == "G3":
                S.emit()
                return nc, S
            for d_ in range(2):
                set_state(d_, sin_[:, d_], "sin")
            scan(LG, True, False)
            if stop == "G4":
                S.emit()
                return nc, S
            ggl = A("ggl", [64, 64], F32)
            DMA("sp", ggl[:], g_gla[l, :].partition_broadcast(64), (), ["ggl"])
            fsq = A("fsq", [64, 256], F32)
            fss = A("fss", [64, 8], F32)
            fo = A("fo", [64, 256], F32)
            fob = A("fob", [64, 256], BF16)
            gch = Ring([A("gch%d" % i, [64, 8, 256], F32) for i in range(2)], "gch")
            mst = Ring([A("mstg%d" % i, [128, 2, 512], BF16) for i in range(2)], "mstg")
            for (t0, n, is_ctx) in (LG if last else GRPS):
                nch = n // 64
                gc_, gck = gch.next()
                ms_, msk_ = mst.next()
                DMA("sp", gc_[:, 0:nch, :], gg_d[t0:t0 + n, :].rearrange("(c p) f -> p c f", p=64), tk("gg_d", t0, n), [gck])
                for c in range(nch):
                    gci = t0 // 64 + c
                    v3 = lambda a: a.rearrange("p (h d) -> p h d", d=64)
                    ACT(fsq[:], oacc[:, gci, :], AF.Square, ["oacc"], ["fsq"])
                    OP("dve", lambda: nc.vector.tensor_reduce(out=fss[:, 0:4], in_=v3(fsq[:]), axis=AX.X, op=ALU.add), ["fsq"], ["fssss"])
                    rms_rstd(None, fss[:, 0:4], fss[:, 4:8], 64, "fss")
                    TT("dve", v3(fo[:]), v3(oacc[:, gci, :]), fss[:, 4:8].unsqueeze(2).broadcast_to([64, 4, 64]), ALU.mult,
                       ["oacc", "fssr"], ["fo"])
                    TT("pool", v3(fo[:]), v3(fo[:]), ggl[:, :].unsqueeze(1).broadcast_to([64, 4, 64]), ALU.mult, ["fo", "ggl"], ["fo"])
                    TT("dve", fob[:], fo[:], gc_[:, c, :], ALU.mult, ["fo", gck], ["fob"])
                    for pr in range(2):
                        TR(PB(0)[:, pr * 64:(pr + 1) * 64], fob[:, pr * 128:(pr + 1) * 128], idb[0:64, 0:64], ["fob", "idb"], ["pb0"])
                    CP("act", ms_[:, :, c * 64:(c + 1) * 64], PB(0)[:, 0:128].rearrange("p (t s) -> p t s", t=2), ["pb0"], [msk_])
                DMA("sp", mixT_d[256:512, t0:t0 + n].rearrange("(t p) s -> p t s", p=128), ms_[:, :, 0:n], [msk_], tk("mixG", t0, n))
            S.emit()
        if stop == "G":
            return nc, S

        with ExitStack() as st:
            A = lambda n, s, d: st.enter_context(nc.sbuf_tensor("%s_u%d" % (n, next(uid)), s, d))
            wblk = A("wblk", [128, 2, 128], BF16)
            MEMSET("dve", wblk[:], 0.0, ["wblk"])
            for g in range(4):
                bs = 64 * (g % 2)
                DMA("pool", wblk[bs:bs + 64, g // 2, bs:bs + 64], w_pool[l, g], ["wblk"], ["wblk"])
            spl = A("spl", [128, 2], F32)
            DMA("sp", spl[:], s_pool[l].rearrange("(t p) -> p t", p=128), (), ["spl"], allow_slow_non_contiguous=True)
            NE = 528
            Er = Ring([A("E%d" % i, [128, 2, NE], F32) for i in range(2)], "E")
            rcr = Ring([A("prc%d" % i, [128, 2, 512], F32) for i in range(2)], "prc")
            Wl = [A("Wl%d" % i, [128, 2, NE], F32) for i in range(4)]
            tmpm = A("tmpm", [128, 2, 512], F32)
            mb = A("mb", [128, 2, 512], BF16)
            yst = Ring([A("yst%d" % i, [128, 2, 512], BF16) for i in range(2)], "yst")
            for (t0, n, is_ctx) in (LG if last else GRPS):
                E, Ek = Er.next()
                rc_, rk = rcr.next()
                e0 = PE0 if is_ctx else PE1 + (t0 - CTX)
                DMA("sp", E[:, :, 0:n + 16], pTe_d[:, e0:e0 + n + 16].rearrange("(t p) s -> p t s", p=128), ["pTe_d"], [Ek])
                DMA("sp", rc_[:, :, 0:n], pool_rc[:, :, t0:t0 + n].rearrange("t p s -> p t s"), (), [rk])
                ne = n + 16
                TT("dve", Wl[0][:, :, 1:ne], E[:, :, 0:ne - 1], E[:, :, 1:ne], ALU.add, [Ek], ["Wl0"])
                TT("pool", Wl[1][:, :, 2:ne - 1], Wl[0][:, :, 1:ne - 2], Wl[0][:, :, 3:ne], ALU.add, ["Wl0"], ["Wl1"])
                TT("dve", Wl[2][:, 1, 4:ne - 3], Wl[1][:, 1, 2:ne - 5], Wl[1][:, 1, 6:ne - 1], ALU.add, ["Wl1"], ["Wl2"])
                TT("pool", Wl[3][:, 1, 8:ne - 8], Wl[2][:, 1, 4:ne - 12], Wl[2][:, 1, 12:ne - 4], ALU.add, ["Wl2"], ["Wl3"])
                for t in range(2):
                    for hf in range(2):
                        ps_ = slice(64 * hf, 64 * hf + 64)
                        lv = 2 * t + hf
                        eng = "dve" if hf == 0 else "pool"
                        TT(eng, tmpm[ps_, t, 0:n], Wl[lv][ps_, t, 8:8 + n], rc_[ps_, t, 0:n], ALU.mult,
                           ["Wl%d" % lv, rk], ["tmpm%d%d" % (t, hf)])
                        TT(eng, mb[ps_, t, 0:n], tmpm[ps_, t, 0:n], E[ps_, t, 8:8 + n], ALU.subtract,
                           ["tmpm%d%d" % (t, hf), Ek], ["mb%d%d" % (t, hf)])
                y_, yk = yst.next()
                for t in range(2):
                    MM(PF(t)[:, 0:n], wblk[:, t, :], mb[:, t, 0:n], True, True, ["wblk", "mb%d0" % t, "mb%d1" % t], ["pf%d" % t])
                    ACT(y_[:, t, 0:n], PF(t)[:, 0:n], AF.Copy, ["pf%d" % t, "spl"], [yk], scale=spl[:, t:t + 1])
                DMA("sp", mixT_d[0:256, t0:t0 + n].rearrange("(t p) s -> p t s", p=128), y_[:, :, 0:n], [yk], tk("mixP", t0, n))
            S.emit()
        if stop == "P":
            return nc, S

        with ExitStack() as st:
            A = lambda n, s, d: st.enter_context(nc.sbuf_tensor("%s_u%d" % (n, next(uid)), s, d))
            KT = A("KT", [128, NK], BF16)
            V1 = A("V1", [128, NKT, 132], BF16)
            DMA("sp", KT[:, 0:CTX], KTc_d[:, :], ["KTc_d"], ["KT"])
            DMA("sp", V1[:, 0:2, :], Vc_d.rearrange("(t p) e -> p t e", p=128), ["Vc_d"], ["V1"])
            for gi in range(NG):
                for r in range(4):
                    k0 = CTX + (gi * 4 + r) * 512
                    DMA("sp", KT[:, k0:k0 + 512], ccK_out[gi][r * 128:(r + 1) * 128, :], ["ccK_out%d" % gi], ["KT"])
                DMA("sp", V1[:, 2 + gi * 16:2 + (gi + 1) * 16, :], ccV_out[gi].rearrange("(t p) e -> p t e", p=128),
                    ["ccV_out%d" % gi], ["V1"])
            QTr = Ring([A("QTt%d" % i, [128, 512], BF16) for i in range(2)], "QTt")
            Pr = Ring([A("Pt%d" % i, [128, 512], BF16) for i in range(4)], "Pt")
            Osb = Ring([A("Osb%d" % i, [65, 512], F32) for i in range(2)], "Osb")
            atr = Ring([A("at%d" % i, [128, 512], BF16) for i in range(2)], "at")
            rcp = A("rcp", [128, 4], F32)
            mst = Ring([A("mstb%d" % i, [128, 4, 128], BF16) for i in range(2)], "mstb")
            qtiles = [(t0 + j * 128, is_ctx) for (t0, n, is_ctx) in (LG if last else GRPS) for j in range(n // 128)]
            sb = 0
            for (tq, is_ctx) in qtiles:
                nkt = CTX // 128 if is_ctx else NKT
                Q_, Qk = QTr.next()
                DMA("sp", Q_[:].rearrange("p (g t) -> p g t", g=4), QT_d[:, :, tq:tq + 128], tk("QT_d", tq), [Qk])
                at_, atk = atr.next()
                for kvh in range(2):
                    bs = 64 * kvh
                    po = 3 + kvh
                    pend = []

                    def qk(kt):
                        nonlocal sb
                        i = sb % 3
                        sb += 1
                        MM(PF(i), KT[bs:bs + 64, kt * 128:(kt + 1) * 128], Q_[bs:bs + 64, :], True, True, ["KT", Qk], ["pf%d" % i])
                        pend.append(i)

                    qk(0)
                    if nkt > 1:
                        qk(1)
                    for kt in range(nkt):
                        i = pend.pop(0)
                        P_, Pk = Pr.next()
                        ACT(P_[:], PF(i), AF.Exp, ["pf%d" % i], [Pk], scale=0.125)
                        MM(PF(po)[0:65, :], V1[:, kt, kvh * 66:kvh * 66 + 65], P_[:], kt == 0, kt == nkt - 1, ["V1", Pk], ["pf%d" % po])
                        if kt + 2 < nkt:
                            qk(kt + 2)
                    O_, Ok = Osb.next()
                    CP("dve", O_[:], PF(po)[0:65, :], ["pf%d" % po], [Ok])
                    for g in range(4):
                        TR(PF(5)[:, g * 65:(g + 1) * 65], O_[0:65, g * 128:(g + 1) * 128], idf[0:65, 0:65], [Ok, "idf"], ["pf5"])
                    p5 = PF(5)[:, 0:260].rearrange("p (g e) -> p g e", g=4)
                    OP("dve", lambda p5=p5: nc.vector.reciprocal(out=rcp[:].unsqueeze(2), in_=p5[:, :, 64:65]), ["pf5"], ["rcp"])
                    TT("dve", at_[:, kvh * 256:(kvh + 1) * 256].rearrange("p (g d) -> p g d", g=4), p5[:, :, 0:64],
                       rcp[:].unsqueeze(2).broadcast_to([128, 4, 64]), ALU.mult, ["pf5", "rcp"], [atk + "_%d" % kvh])
                for j in range(4):
                    TR(PB(0)[:, j * 128:(j + 1) * 128], at_[:, j * 128:(j + 1) * 128], idb[:], [atk + "_0", atk + "_1", "idb"], ["pb0"])
                m_, mk = mst.next()
                CP("act", m_[:], PB(0)[:, 0:512].rearrange("p (j t) -> p j t", j=4), ["pb0"], [mk])
                DMA("sp", mixT_d[512:1024, tq:tq + 128].rearrange("(j p) t -> p j t", p=128), m_[:], [mk], tk("mixB", tq))
            S.emit()
        if stop == "B":
            return nc, S

        cg = LG if last else GRPS
        with ExitStack() as st:
            A = lambda n, s, d: st.enter_context(nc.sbuf_tensor("%s_u%d" % (n, next(uid)), s, d))
            wo = A("wo", [128, 8, D], BF16)
            for k in range(8):
                DMA("pool", wo[:, k, :], w_out[l, k * 128:(k + 1) * 128, :], (), ["wo"])
            bc = {}
            for si in range(2):
                for nm, c0 in (("ga1", 2 * D), ("sh2", 3 * D), ("gm2", 4 * D)):
                    t = A("%s_%d" % (nm, si), [128, D], F32)
                    bcast_load(t, mod_d[si, c0:c0 + D], "%s_%d" % (nm, si))
                    bc[(nm, si)] = t
            mTr = Ring([A("mT%d" % i, [128, 8, 512], BF16) for i in range(2)], "mT")
            xr = Ring([A("xc%d" % i, [128, D], F32) for i in range(3)], "xc")
            tmp = A("tmpc", [128, D], F32)
            junk = A("junkc", [128, D], BF16)
            st2 = A("st2c", [128, 8], F32)
            h1 = A("h1c", [128, D], F32)
            hb = A("hbc", [128, D], BF16)
            hst = Ring([A("hst%d" % i, [128, 8, 128], BF16) for i in range(2)], "hst")
            for (t0, n, is_ctx) in cg:
                si = 1 if is_ctx else 0
                mT, mTk = mTr.next()
                DMA("sp", mT[:, :, 0:n], mixT_d[:, t0:t0 + n].rearrange("(k p) t -> p k t", p=128), tk("mixG", t0, n) + tk("mixP", t0, n) + tk("mixB", t0, n), [mTk])
                for j in range(n // 128):
                    r0 = t0 + j * 128
                    xt, xk = xr.next()
                    DMA("sp", xt[:], x_d[r0:r0 + 128, :], tk("x_d", r0), [xk])
                    for hf in range(2):
                        for k in range(8):
                            MM(PF(hf), mT[:, k, j * 128:(j + 1) * 128], wo[:, k, hf * 512:(hf + 1) * 512], k == 0, k == 7,
                               [mTk, "wo"], ["pf%d" % hf])
                        hs = slice(hf * 512, (hf + 1) * 512)
                        TT("dve", tmp[:, hs], PF(hf), bc[("ga1", si)][:, hs], ALU.mult, ["pf%d" % hf, "ga1_%d" % si], ["tmpc%d" % hf])
                        TT("pool", xt[:, hs], xt[:, hs], tmp[:, hs], ALU.add, [xk, "tmpc%d" % hf], [xk])
                    DMA("sp", x_d[r0:r0 + 128, :], xt[:], [xk], tk("x_d", r0))
                    MEMSET("pool", st2[:, 0:1], 0.0, ["st2css"])
                    ACT(junk[:], xt[:], AF.Square, [xk, "st2css"], ["junkc", "st2css"], accum_out=st2[:, 0:1])
                    rms_rstd(None, st2[:, 0:1], st2[:, 1:2], D, "st2c")
                    STT(h1[:], xt[:], st2[:, 1:2], bc[("gm2", si)][:], ALU.mult, ALU.mult, [xk, "st2cr", "gm2_%d" % si], ["h1c"])
                    TT("pool", hb[:], h1[:], bc[("sh2", si)][:], ALU.add, ["h1c", "sh2_%d" % si], ["hbc"])
                    for k in range(8):
                        TR(PB(0)[:, k * 128:(k + 1) * 128], hb[:, k * 128:(k + 1) * 128], idb[:], ["hbc", "idb"], ["pb0"])
                    hs_, hsk = hst.next()
                    CP("act", hs_[:], PB(0).rearrange("p (k t) -> p k t", k=8), ["pb0"], [hsk])
                    DMA("sp", h2T_d[:, r0:r0 + 128].rearrange("(k p) t -> p k t", p=128), hs_[:], [hsk], tk("h2T_d", r0))
            S.emit()
        if stop == "C1":
            return nc, S

        for hh in range(2):
            with ExitStack() as st:
                A = lambda n, s, d: st.enter_context(nc.sbuf_tensor("%s_u%d" % (n, next(uid)), s, d))
                wu = A("wu", [128, 8, 2048], BF16)
                wd = A("wd", [128, 16, D], BF16)
                for k in range(8):
                    DMA("pool", wu[:, k, :], w_up[l, k * 128:(k + 1) * 128, hh * 2048:(hh + 1) * 2048], (), ["wu"])
                for c in range(16):
                    DMA("pool", wd[:, c, :], w_down[l, hh * 2048 + c * 128:hh * 2048 + (c + 1) * 128, :], (), ["wd"])
                bc = {}
                for si in range(2):
                    t = A("ga2_%d" % si, [128, D], F32)
                    bcast_load(t, mod_d[si, 5 * D:6 * D], "ga2_%d" % si)
                    bc[si] = t
                gfin = None
                if last and hh == 1:
                    gfin = A("gfin", [128, D], F32)
                    DMA("sp", gfin[:], g_final[0, :].partition_broadcast(128), (), ["gfin"])
                h2r = Ring([A("h2T%d" % i, [128, 8, 512], BF16) for i in range(2)], "h2T")
                hid = A("hid", [128, 16, 512], BF16)
                rl = Ring([A("rl%d" % i, [128, 512], F32) for i in range(3)], "rl")
                xr = Ring([A("xm%d" % i, [128, D], F32) for i in range(3)], "xm")
                tmp = A("tmpd", [128, D], F32)
                junk = A("junkd", [128, D], BF16)
                st2 = A("st2d", [128, 8], F32)
                for (t0, n, is_ctx) in cg:
                    si = 1 if is_ctx else 0
                    h2, h2k = h2r.next()
                    DMA("sp", h2[:, :, 0:n], h2T_d[:, t0:t0 + n].rearrange("(k p) t -> p k t", p=128), tk("h2T_d", t0, n), [h2k])
                    for c in range(16):
                        pi = c % 3
                        for k in range(8):
                            MM(PF(pi)[:, 0:n], wu[:, k, c * 128:(c + 1) * 128], h2[:, k, 0:n], k == 0, k == 7, ["wu", h2k], ["pf%d" % pi])
                        r_, rk = rl.next()
                        ACT(r_[:, 0:n], PF(pi)[:, 0:n], AF.Relu, ["pf%d" % pi], [rk])
                        TT("pool", hid[:, c, 0:n], r_[:, 0:n], r_[:, 0:n], ALU.mult, [rk], ["hid%d" % c])
                    hidk = ["hid%d" % c for c in range(16)]
                    for j in range(n // 128):
                        r0 = t0 + j * 128
                        xt, xk = xr.next()
                        DMA("sp", xt[:], x_d[r0:r0 + 128, :], tk("x_d", r0), [xk])
                        for hf in range(2):
                            pi = 3 + hf
                            for c in range(16):
                                MM(PF(pi), hid[:, c, j * 128:(j + 1) * 128], wd[:, c, hf * 512:(hf + 1) * 512], c == 0, c == 15,
                                   hidk + ["wd"], ["pf%d" % pi])
                            hs = slice(hf * 512, (hf + 1) * 512)
                            TT("dve", tmp[:, hs], PF(pi), bc[si][:, hs], ALU.mult, ["pf%d" % pi, "ga2_%d" % si], ["tmpd%d" % hf])
                            TT("pool", xt[:, hs], xt[:, hs], tmp[:, hs], ALU.add, [xk, "tmpd%d" % hf], [xk])
                        if gfin is not None:
                            MEMSET("pool", st2[:, 0:1], 0.0, ["st2dss"])
                            ACT(junk[:], xt[:], AF.Square, [xk, "st2dss"], ["junkd", "st2dss"], accum_out=st2[:, 0:1])
                            rms_rstd(None, st2[:, 0:1], st2[:, 1:2], D, "st2d")
                            STT(xt[:], xt[:], st2[:, 1:2], gfin[:], ALU.mult, ALU.mult, [xk, "st2dr", "gfin"], [xk])
                            DMA("sp", out[r0 - CTX:r0 - CTX + 128, :], xt[:], [xk], tk("out", r0))
                        else:
                            DMA("sp", x_d[r0:r0 + 128, :], xt[:], [xk], tk("x_d", r0))
                if last and hh == 1:
                    S.final_wait("sp", tk("out", CTX, TL))
                S.emit()
    return nc, S


def _tables(SEQ, TL, r):
    half = 32
    inv = (10000.0 ** (-np.arange(0, half, 2, dtype=np.float32) / half)).astype(np.float32)
    tg = np.arange(r * TL, (r + 1) * TL)
    rows = (tg // 64).astype(np.float32)
    cols = (tg % 64).astype(np.float32)
    C = np.zeros((TL, 64), np.float32)
    Sg = np.zeros((TL, 64), np.float32)
    for hi, pos in enumerate((rows, cols)):
        ang = (pos[:, None] * inv[None, :]).astype(np.float32)
        c, s = np.cos(ang).astype(np.float32), np.sin(ang).astype(np.float32)
        o = 32 * hi
        C[:, o:o + 16] = c
        C[:, o + 16:o + 32] = c
        Sg[:, o:o + 16] = -s
        Sg[:, o + 16:o + 32] = s
    T = CTX + TL
    rc = np.zeros((2, 128, T), np.float32)
    for t in range(2):
        for hf in range(2):
            w = WIN[2 * t + hf]
            for (off, N, pos) in ((0, CTX, np.arange(CTX)), (CTX, SEQ, tg)):
                lo = np.clip(pos - w // 2, 0, N - 1)
                hi_ = np.clip(pos + w // 2 - 1, 0, N - 1)
                cnt = (hi_ - lo + 1).astype(np.float32)
                rc[t, 64 * hf:64 * hf + 64, off:off + len(pos)] = (1.0 / cnt)[None, :]
    sel = np.zeros((128, 12), np.float32)
    sel[:, r] = 1.0
    if r > 0:
        sel[:, 4 + r - 1] = 1.0
    if r < 3:
        sel[:, 8 + r + 1] = 1.0
    return C, Sg, rc, sel


def make_in_maps(inp, SEQ, NL):
    TL = SEQ // 4
    f = lambda a: np.ascontiguousarray(np.asarray(a, dtype=np.float32))
    shared = {k: f(inp[k])[:NL] for k in ("w_mod", "b_mod", "g_mix", "w_in", "w_pool", "s_pool", "w_gate", "b_gate",
                                           "g_gla", "g_q", "g_k", "w_out", "g_mlp", "w_up", "w_down")}
    shared["g_final"] = f(inp["g_final"]).reshape(1, D)
    x = f(inp["x"])
    ctx = f(inp["ctx"])
    c = f(inp["c"])
    cc = f(inp["c_ctx"])
    maps = []
    for core in range(8):
        b, r = core // 4, core % 4
        C, Sg, rc, sel = _tables(SEQ, TL, r)
        m = dict(shared)
        m["x"] = np.ascontiguousarray(x[b, r * TL:(r + 1) * TL])
        m["ctx"] = np.ascontiguousarray(ctx[b])
        m["cvec"] = np.ascontiguousarray(np.stack([c[b], cc]))
        m["rope_c"] = C
        m["rope_s"] = Sg
        m["pool_rc"] = rc
        m["sel"] = sel
        maps.append(m)
    return maps


_NC_CACHE = {}


def run(inp, SEQ, NL, debug=(), stop=None):
    TL = SEQ // 4
    key = (TL, NL, tuple(debug), stop)
    if key not in _NC_CACHE:
        _NC_CACHE[key] = build(TL, NL, debug, stop)[0]
    nc = _NC_CACHE[key]
    res = run_bass_kernel_spmd(nc, make_in_maps(inp, SEQ, NL), core_ids=list(range(8)))
    out = np.zeros((2, SEQ, D), np.float32)
    for core in range(8):
        b, r = core // 4, core % 4
        out[b, r * TL:(r + 1) * TL] = res.results[core]["out"]
    return out, res


def kernel(**inputs):
    SEQ = inputs["x"].shape[1]
    NL = inputs["w_mod"].shape[0]
    return run(inputs, SEQ, NL)[0]
```
